# Optimizing a Trainium2 kernel written in Bass

```python
import jax, jax.numpy as jnp
from jax import lax
import numpy as np

D_MODEL = 1024
BATCH = 2
SEQ = 8192
DEPTH = 2

N_MIXERS = 2
NORM_EPS = 1e-5
DILATED_GROUPS = ((128, 1), (512, 4), (2048, 16))
N_GROUPS = len(DILATED_GROUPS)
ATTN_HEADS = 8
HEAD_DIM = D_MODEL // ATTN_HEADS
ATTN_WIDTH = ATTN_HEADS * HEAD_DIM
ATTN_IN = N_GROUPS * 3 * ATTN_WIDTH
ATTN_BLOCK = 128
ROPE_THETA = 10000.0
GLA_HEADS = 4
GLA_KEY_DIM = D_MODEL // 2
GLA_VAL_DIM = D_MODEL
GLA_DK = GLA_KEY_DIM // GLA_HEADS
GLA_DV = GLA_VAL_DIM // GLA_HEADS
GLA_GATE_RANK = 16
GLA_GATE_NORMALIZER = 16.0
GLA_CHUNK = 64
GLA_IN = 2 * GLA_KEY_DIM + 2 * GLA_VAL_DIM + GLA_GATE_RANK
D_FF = 4 * D_MODEL

kernel_name = "hybrid_dilated_attn_gla_sqrelu"


def rmsnorm(x, w):
    xf = x.astype(jnp.float32)
    y = xf * lax.rsqrt(jnp.mean(xf * xf, axis=-1, keepdims=True) + NORM_EPS)
    return (y * w.astype(jnp.float32)).astype(x.dtype)


def rope_tables(seq_len, dim):
    pos = jnp.arange(seq_len, dtype=jnp.float32)
    inv_freq = ROPE_THETA ** (-jnp.arange(0, dim, 2, dtype=jnp.float32) / dim)
    ang = pos[:, None] * inv_freq[None, :]
    return jnp.cos(ang), jnp.sin(ang)


def apply_rope(t, cos, sin):
    tf = t.astype(jnp.float32)
    half = tf.shape[-1] // 2
    t1, t2 = tf[..., :half], tf[..., half:]
    c = cos[None, :, None, :]
    s = sin[None, :, None, :]
    return jnp.concatenate([t1 * c - t2 * s, t2 * c + t1 * s], axis=-1)


def dilated_group_attention(q, k, v, dil, n_back):
    B, S, H, D = q.shape
    L = S // dil

    def to_res(t):
        return t.reshape(B, L, dil, H, D).transpose(0, 2, 3, 1, 4)

    C = min(ATTN_BLOCK, L)
    nb = -(-L // C)
    Lp = nb * C
    K = n_back + C
    qr = jnp.pad(to_res(q), ((0, 0), (0, 0), (0, 0), (0, Lp - L), (0, 0)))
    qr = qr.reshape(B, dil, H, nb, C, D)
    kv_pad = ((0, 0), (0, 0), (0, 0), (n_back, Lp - L), (0, 0))
    kr = jnp.pad(to_res(k), kv_pad)
    vr = jnp.pad(to_res(v), kv_pad)
    idx = jnp.arange(nb)[:, None] * C + jnp.arange(K)[None, :]
    kb = kr[:, :, :, idx, :]
    vb = vr[:, :, :, idx, :]
    s = jnp.einsum('bzhnqd,bzhnkd->bzhnqk', qr, kb)
    qi = jnp.arange(C)[:, None]
    kj = jnp.arange(K)[None, :]
    rel = qi - kj + n_back
    key_idx = jnp.arange(nb)[:, None, None] * C + kj[None] - n_back
    mask = (rel >= 0) & (rel <= n_back) & (key_idx >= 0)
    s = jnp.where(mask, s, -jnp.inf)
    m = jnp.max(s, axis=-1, keepdims=True)
    p = jnp.exp(s - m)
    l = jnp.sum(p, axis=-1)
    o = jnp.einsum('bzhnqk,bzhnkd->bzhnqd', p, vb) / l[..., None]
    lse = m[..., 0] + jnp.log(l)
    o = o.reshape(B, dil, H, Lp, D)[:, :, :, :L]
    o = o.transpose(0, 3, 1, 2, 4).reshape(B, S, H, D)
    lse = lse.reshape(B, dil, H, Lp)[:, :, :, :L]
    lse = lse.transpose(0, 3, 1, 2).reshape(B, S, H)
    return o, lse


def dilated_attention(h, w_in, w_out):
    B, S, _ = h.shape
    qkv = (h @ w_in).reshape(B, S, N_GROUPS, 3, ATTN_HEADS, HEAD_DIM)
    cos, sin = rope_tables(S, HEAD_DIM)
    outs, lses = [], []
    for g, (window, dil) in enumerate(DILATED_GROUPS):
        q = apply_rope(qkv[:, :, g, 0], cos, sin) * (HEAD_DIM ** -0.5)
        k = apply_rope(qkv[:, :, g, 1], cos, sin)
        v = qkv[:, :, g, 2].astype(jnp.float32)
        o_g, lse_g = dilated_group_attention(q, k, v, dil, window // dil)
        outs.append(o_g)
        lses.append(lse_g)
    alpha = jax.nn.softmax(jnp.stack(lses, axis=0), axis=0)
    o = jnp.sum(alpha[..., None] * jnp.stack(outs, axis=0), axis=0)
    return o.reshape(B, S, ATTN_WIDTH).astype(h.dtype) @ w_out


def gla_mixer(h, w_in, w_gate_up, b_gate, norm_w, w_out):
    B, S, _ = h.shape
    nc = S // GLA_CHUNK
    proj = h @ w_in
    s1 = GLA_KEY_DIM
    s2 = 2 * GLA_KEY_DIM
    s3 = s2 + GLA_VAL_DIM
    s4 = s3 + GLA_VAL_DIM
    q, k, v = proj[..., :s1], proj[..., s1:s2], proj[..., s2:s3]
    g_out, gate_lr = proj[..., s3:s4], proj[..., s4:]
    gk = gate_lr @ w_gate_up + b_gate
    log_a = jax.nn.log_sigmoid(gk.astype(jnp.float32)) / GLA_GATE_NORMALIZER

    def chunked(t, d):
        t = t.astype(jnp.float32).reshape(B, nc, GLA_CHUNK, GLA_HEADS, d)
        return t.transpose(0, 3, 1, 2, 4)

    qc = chunked(q, GLA_DK) * (GLA_DK ** -0.5)
    kc = chunked(k, GLA_DK)
    vc = chunked(v, GLA_DV)
    b = jnp.cumsum(chunked(log_a, GLA_DK), axis=-2)
    b_last = b[..., -1:, :]
    q_t = qc * jnp.exp(b)
    k_t = kc * jnp.exp(-b)
    causal = jnp.tril(jnp.ones((GLA_CHUNK, GLA_CHUNK), dtype=bool))
    A = jnp.where(causal, jnp.einsum('bhncd,bhnsd->bhncs', q_t, k_t), 0.0)
    o_intra = jnp.einsum('bhncs,bhnsv->bhncv', A, vc)
    U = jnp.einsum('bhncd,bhncv->bhndv', kc * jnp.exp(b_last - b), vc)
    decay = jnp.exp(b_last[..., 0, :])

    def step(state, inp):
        u, a = inp
        return a[..., None] * state + u, state

    init = jnp.zeros((B, GLA_HEADS, GLA_DK, GLA_DV), jnp.float32)
    _, s_prev = lax.scan(step, init, (jnp.moveaxis(U, 2, 0), jnp.moveaxis(decay, 2, 0)))
    s_prev = jnp.moveaxis(s_prev, 0, 2)
    o = o_intra + jnp.einsum('bhncd,bhndv->bhncv', q_t, s_prev)
    o = o.transpose(0, 2, 3, 1, 4).reshape(B, S, GLA_HEADS, GLA_DV)
    o = rmsnorm(o, norm_w)
    gate = jax.nn.silu(g_out.astype(jnp.float32)).reshape(B, S, GLA_HEADS, GLA_DV)
    o = (o * gate).reshape(B, S, GLA_VAL_DIM).astype(h.dtype)
    return o @ w_out


def sq_relu_mlp(h, w_up, w_down):
    a = jax.nn.relu(h @ w_up)
    return (a * a) @ w_down


def setup_inputs(seed: int = 0) -> dict:
    key = jax.random.key(seed)
    ks = jax.random.split(key, 16)
    n_attn = (DEPTH + 1) // N_MIXERS
    n_gla = DEPTH // N_MIXERS

    def w(k, shape, fan_in):
        return jax.random.normal(k, shape, jnp.float32) * (fan_in ** -0.5)

    def gain(k, shape):
        return 1.0 + 0.02 * jax.random.normal(k, shape, jnp.float32)

    return {
        "x": jax.random.normal(ks[0], (BATCH, SEQ, D_MODEL), jnp.float32),
        "norm_mix_w": gain(ks[1], (DEPTH, D_MODEL)),
        "norm_mlp_w": gain(ks[2], (DEPTH, D_MODEL)),
        "final_norm_w": gain(ks[3], (D_MODEL,)),
        "attn_w_in": w(ks[4], (n_attn, D_MODEL, ATTN_IN), D_MODEL),
        "attn_w_out": w(ks[5], (n_attn, ATTN_WIDTH, D_MODEL), ATTN_WIDTH),
        "gla_w_in": w(ks[6], (n_gla, D_MODEL, GLA_IN), D_MODEL),
        "gla_w_gate_up": w(ks[7], (n_gla, GLA_GATE_RANK, GLA_KEY_DIM), GLA_GATE_RANK),
        "gla_b_gate": 0.02 * jax.random.normal(ks[8], (n_gla, GLA_KEY_DIM), jnp.float32),
        "gla_norm_w": gain(ks[9], (n_gla, GLA_DV)),
        "gla_w_out": w(ks[10], (n_gla, GLA_VAL_DIM, D_MODEL), GLA_VAL_DIM),
        "mlp_w_up": w(ks[11], (DEPTH, D_MODEL, D_FF), D_MODEL),
        "mlp_w_down": w(ks[12], (DEPTH, D_FF, D_MODEL), D_FF),
    }


def reference(x, norm_mix_w, norm_mlp_w, final_norm_w, attn_w_in, attn_w_out,
              gla_w_in, gla_w_gate_up, gla_b_gate, gla_norm_w, gla_w_out,
              mlp_w_up, mlp_w_down):
    for i in range(DEPTH):
        h = rmsnorm(x, norm_mix_w[i])
        j = i // N_MIXERS
        if i % N_MIXERS == 0:
            x = x + dilated_attention(h, attn_w_in[j], attn_w_out[j])
        else:
            x = x + gla_mixer(h, gla_w_in[j], gla_w_gate_up[j], gla_b_gate[j],
                              gla_norm_w[j], gla_w_out[j])
        h = rmsnorm(x, norm_mlp_w[i])
        x = x + sq_relu_mlp(h, mlp_w_up[i], mlp_w_down[i])
    return rmsnorm(x, final_norm_w)
```

```python
from contextlib import ExitStack
import numpy as np
import concourse.bass as bass
import concourse.mybir as mybir
from concourse.bass_utils import run_bass_kernel_spmd

F32 = mybir.dt.float32
BF16 = mybir.dt.bfloat16
AF = mybir.ActivationFunctionType
ALU = mybir.AluOpType
AX = mybir.AxisListType

D = 1024
NT = 16
TOK = 2048
EPS = 1e-5
NEG = -30000.0


class Buf:
    __slots__ = ("name", "w", "r", "dsem", "dcnt", "excl")

    def __init__(self, name, excl=False):
        self.name = name
        self.excl = excl
        self.w = None
        self.r = []
        self.dsem = None
        self.dcnt = 0


class Sched:
    ENGS = ("sync", "scalar", "vector", "gpsimd", "tensor")

    def __init__(self, nc, stack, same_engine_sync=True):
        self.nc = nc
        self.stack = stack
        self.ops = {e: [] for e in self.ENGS}
        self.sem = {}
        self.cnt = {e: 0 for e in self.ENGS}
        self.waited = {e: {} for e in self.ENGS}
        self.same = same_engine_sync
        self.nsem = 0
        self.dma_sems = []
        self.free_dsems = []
        for e in ("scalar", "vector", "gpsimd", "tensor"):
            self.sem[e] = self.new_sem("p_" + e)
        self.pending_unsig = {e: False for e in self.ENGS}

    def new_sem(self, name):
        self.nsem += 1
        s = self.stack.enter_context(self.nc.semaphore(f"{name}_{self.nsem}"))
        return s

    def _need(self, eng, tok):
        if tok is None:
            return
        sem, c, src = tok
        if src == eng and (eng == "tensor" or not self.same):
            return
        w = self.waited[eng]
        if w.get(id(sem), 0) >= c:
            return
        w[id(sem)] = c
        self.ops[eng].append(lambda e, sem=sem, c=c: e.wait_ge(sem, c))

    def _deps(self, eng, reads, writes):
        for b in reads:
            self._need(eng, b.w)
            if b.excl:
                for t in b.r:
                    if t[2] != eng:
                        self._need(eng, t)
        for b in writes:
            self._need(eng, b.w)
            for t in b.r:
                self._need(eng, t)

    def _post(self, tok, reads, writes):
        for b in reads:
            b.r.append(tok)
        for b in writes:
            b.w = tok
            b.r = []

    def op(self, eng, fn, reads=(), writes=(), signal=True):
        self._deps(eng, reads, writes)
        sem = self.sem[eng]
        if signal:
            self.cnt[eng] += 1
            c = self.cnt[eng]
            self.ops[eng].append(lambda e, fn=fn, sem=sem: fn(e).then_inc(sem, 1))
            self.pending_unsig[eng] = False
        else:
            c = self.cnt[eng] + 1
            self.ops[eng].append(lambda e, fn=fn: fn(e))
            self.pending_unsig[eng] = True
        tok = (sem, c, eng)
        self._post(tok, reads, writes)
        return tok

    def dma(self, q, out_ap, in_ap, reads=(), writes=(), owner=None, **kw):
        self._deps(q, reads, writes)
        if owner is None:
            owner = reads[0] if reads else writes[0]
        if owner.dsem is None:
            owner.dsem = self.new_sem("d_" + owner.name)
            self.dma_sems.append(owner)
        owner.dcnt += 16
        sem, c = owner.dsem, owner.dcnt
        self.ops[q].append(lambda e, o=out_ap, i=in_ap, sem=sem, kw=kw: e.dma_start(out=o, in_=i, **kw).then_inc(sem, 16))
        tok = (sem, c, "dma")
        self._post(tok, reads, writes)
        return tok

    def barrier(self):
        toks = []
        for e in ("scalar", "vector", "gpsimd", "tensor"):
            assert not self.pending_unsig[e]
            if self.cnt[e]:
                toks.append((self.sem[e], self.cnt[e], e))
        for b in self.dma_sems:
            toks.append((b.dsem, b.dcnt, "dma"))
        for e in self.ENGS:
            for t in toks:
                if t[2] != e:
                    self._need(e, t)

    def emit(self):
        for e in self.ENGS:
            assert not self.pending_unsig[e], f"engine {e} ends with unsignalled op"
        ops = self.ops
        with self.nc.Block() as block:
            @block.sync
            def _(e):
                for f in ops["sync"]:
                    f(e)

            @block.scalar
            def _(e):
                for f in ops["scalar"]:
                    f(e)

            @block.vector
            def _(e):
                for f in ops["vector"]:
                    f(e)

            @block.gpsimd
            def _(e):
                for f in ops["gpsimd"]:
                    f(e)

            @block.tensor
            def _(e):
                for f in ops["tensor"]:
                    f(e)


class Ring:
    def __init__(self, items):
        self.items = items
        self.i = 0

    def next(self):
        it = self.items[self.i % len(self.items)]
        self.i += 1
        return it


class Ctx:
    pass


def _alloc(C, st, name, shape, dt, psum=False):
    C.uid += 1
    nm = f"{name}_{C.uid}"
    if psum:
        return st.enter_context(C.nc.psum_tensor(nm, shape, dt))
    return st.enter_context(C.nc.sbuf_tensor(nm, shape, dt))


def _ring(C, st, name, shape, dt, n, psum=False):
    return Ring([(_alloc(C, st, name, shape, dt, psum), Buf(f"{name}{i}", excl=psum)) for i in range(n)])


def norm_tile(C, N, src, bsrc, nwT, bnw, dst, bdst):
    S = C.S
    junk, bjunk = N.junk.next()
    ss, bss = N.ss.next()
    xn, bxn = N.xn.next()
    pT, bpT = N.pT.next()
    S.op("scalar", lambda e: e.activation(out=junk[:], in_=src, func=AF.Square, accum_out=ss[:]),
         reads=[bsrc], writes=[bjunk, bss])
    S.op("scalar", lambda e: e.activation(out=ss[:], in_=ss[:], func=AF.Sqrt, scale=1.0 / D, bias=C.eps[:]),
         reads=[bss], writes=[bss])
    S.op("vector", lambda e: e.reciprocal(out=ss[:], in_=ss[:]), reads=[bss], writes=[bss])
    S.op("scalar", lambda e: e.activation(out=xn[:], in_=src, func=AF.Copy, scale=ss[:]),
         reads=[bsrc, bss], writes=[bxn])
    for k in range(8):
        S.op("tensor", lambda e, k=k: e.transpose(out=pT[:, k, :], in_=xn[:, k * 128:(k + 1) * 128], identity=C.identb[:]),
             reads=[bxn], writes=[bpT], signal=(k == 7))
    S.op("vector", lambda e: e.tensor_tensor(out=dst, in0=pT[:], in1=nwT.unsqueeze(2).to_broadcast([128, 8, 128]),
                                             op=ALU.mult),
         reads=[bpT, bnw], writes=[bdst])


def make_norm_res(C, st):
    N = Ctx()
    N.junk = _ring(C, st, "njunk", [128, D], BF16, 1)
    N.ss = _ring(C, st, "nss", [128, 1], F32, 4)
    N.xn = _ring(C, st, "nxn", [128, D], BF16, 2)
    N.pT = _ring(C, st, "npT", [128, 8, 128], BF16, 2, psum=True)
    return N


def load_normw(C, st, dram_vec, name):
    t = _alloc(C, st, name, [128, 8], F32)
    b = Buf(name)
    C.S.dma("sync", t[:], dram_vec.rearrange("(k p) -> p k", p=128), writes=[b], allow_slow_non_contiguous=True)
    return t, b


def phase_mlp(C, li):
    S, nc = C.S, C.nc
    w_up = C.dram["mlp_w_up"]
    w_dn = C.dram["mlp_w_down"]
    with ExitStack() as st:
        N = make_norm_res(C, st)
        nwT, bnw = load_normw(C, st, C.dram["norm_mlp_w"][li], "nw_mlp")
        hT = _alloc(C, st, "hT", [128, 8, TOK], BF16)
        bhT = [Buf(f"hT{t}") for t in range(NT)]
        hid = _alloc(C, st, "hid", [128, 8, TOK], BF16)
        bhid = [Buf(f"hid{g}") for g in range(4)]
        wu = _ring(C, st, "wu", [128, 8, 1024], BF16, 2)
        wd = _ring(C, st, "wd", [128, 8, 1024], BF16, 2)
        r32 = _ring(C, st, "r32", [128, 512], F32, 2)
        pu = _ring(C, st, "pu", [128, 512], F32, 2, psum=True)
        pd = _ring(C, st, "pd", [128, 512], F32, 2, psum=True)

        def load_w(fg):
            wut, bwu = wu.next()
            wdt, bwd = wd.next()
            for k in range(8):
                S.dma("gpsimd", wut[:, k, :], w_up[li, k * 128:(k + 1) * 128, fg * 1024:(fg + 1) * 1024], writes=[bwu])
            for f in range(8):
                r0 = fg * 1024 + f * 128
                S.dma("gpsimd", wdt[:, f, :], w_dn[li, r0:r0 + 128, :], writes=[bwd])
            return wut, bwu, wdt, bwd

        wq = [load_w(0)]
        for t in range(NT):
            norm_tile(C, N, C.x_sb[:, t, :], C.bx[t], nwT[:], bnw, hT[:, :, t * 128:(t + 1) * 128], bhT[t])
        for fg in range(4):
            if fg + 1 < 4:
                wq.append(load_w(fg + 1))
            wut, bwu, wdt, bwd = wq[fg]
            for f in range(8):
                for tg in range(4):
                    bank, bbank = pu.next()
                    for k in range(8):
                        S.op("tensor", lambda e, k=k, f=f, tg=tg, bank=bank, wut=wut: e.matmul(
                            bank[:], lhsT=wut[:, k, f * 128:(f + 1) * 128], rhs=hT[:, k, tg * 512:(tg + 1) * 512],
                            start=(k == 0), stop=(k == 7)),
                            reads=[bwu] + bhT[tg * 4:(tg + 1) * 4], writes=[bbank], signal=(k == 7))
                    r, br = r32.next()
                    S.op("scalar", lambda e, r=r, bank=bank: e.activation(out=r[:], in_=bank[:], func=AF.Relu),
                         reads=[bbank], writes=[br])
                    S.op("vector", lambda e, r=r, f=f, tg=tg: e.tensor_tensor(
                        out=hid[:, f, tg * 512:(tg + 1) * 512], in0=r[:], in1=r[:], op=ALU.mult),
                        reads=[br], writes=[bhid[tg]])
            for t in range(NT):
                for c in range(2):
                    bank, bbank = pd.next()
                    for f in range(8):
                        S.op("tensor", lambda e, f=f, t=t, c=c, bank=bank, wdt=wdt: e.matmul(
                            bank[:], lhsT=hid[:, f, t * 128:(t + 1) * 128], rhs=wdt[:, f, c * 512:(c + 1) * 512],
                            start=(f == 0), stop=(f == 7)),
                            reads=[bwd, bhid[t // 4]], writes=[bbank], signal=(f == 7))
                    xs = C.x_sb[:, t, c * 512:(c + 1) * 512]
                    S.op("vector", lambda e, xs=xs, bank=bank: e.tensor_tensor(out=xs, in0=bank[:], in1=xs, op=ALU.add),
                         reads=[bbank, C.bx[t]], writes=[C.bx[t]])
        S.barrier()


def phase_final(C):
    S, nc = C.S, C.nc
    with ExitStack() as st:
        fw = _alloc(C, st, "fw", [128, D], F32)
        bfw = Buf("fw")
        S.dma("sync", fw[:], C.dram["final_norm_w"].partition_broadcast(128), writes=[bfw])
        junk = _ring(C, st, "fjunk", [128, D], BF16, 1)
        ssr = _ring(C, st, "fss", [128, 1], F32, 4)
        ot = _ring(C, st, "fot", [128, D], F32, 3)
        for t in range(NT):
            jk, bjk = junk.next()
            ss, bss = ssr.next()
            o, bo = ot.next()
            xs = C.x_sb[:, t, :]
            S.op("scalar", lambda e, jk=jk, ss=ss, xs=xs: e.activation(out=jk[:], in_=xs, func=AF.Square, accum_out=ss[:]),
                 reads=[C.bx[t]], writes=[bjk, bss])
            S.op("scalar", lambda e, ss=ss: e.activation(out=ss[:], in_=ss[:], func=AF.Sqrt, scale=1.0 / D, bias=C.eps[:]),
                 reads=[bss], writes=[bss])
            S.op("vector", lambda e, ss=ss: e.reciprocal(out=ss[:], in_=ss[:]), reads=[bss], writes=[bss])
            S.op("vector", lambda e, o=o, ss=ss, xs=xs: e.scalar_tensor_tensor(
                out=o[:], in0=xs, scalar=ss[:], in1=fw[:], op0=ALU.mult, op1=ALU.mult),
                reads=[C.bx[t], bss, bfw], writes=[bo])
            S.dma("sync", C.dram["out"][t * 128:(t + 1) * 128, :], o[:], reads=[bo], writes=[C.bout])
        S.barrier()


DILS = (1, 4, 16)
QSCALE = 128.0 ** -0.5


def phase_attn(C):
    S, nc = C.S, C.nc
    w_in = C.dram["attn_w_in"]
    w_out = C.dram["attn_w_out"]
    with ExitStack() as st:
        hTo = C.x_sb[:, 0:8, :].bitcast(BF16)
        hTh = C.x_sb[:, 8:16, :].bitcast(BF16)
        bhTo = [Buf(f"hTo{t}") for t in range(NT)]
        bhTh = [Buf(f"hTh{t}") for t in range(NT)]
        N = make_norm_res(C, st)
        nwT, bnw = load_normw(C, st, C.dram["norm_mix_w"][0], "nw_mix0")
        oT = _alloc(C, st, "oT", [128, 8, TOK], BF16)
        boT = [Buf(f"oT{h}") for h in range(8)]
        pp = _ring(C, st, "pp", [128, 512], F32, 2, psum=True)

        with ExitStack() as sb:
            def const_bf16(name):
                stg, bstg = cstage.next()
                t = _alloc(C, sb, name, [128, 128], BF16)
                b = Buf(name)
                S.dma("sync", stg[:], C.dram[name], writes=[bstg])
                S.op("vector", lambda e: e.tensor_copy(out=t[:], in_=stg[:]), reads=[bstg], writes=[b])
                return t, b
            cstage = _ring(C, sb, "cstage", [128, 128], F32, 2)
            m0, bm0 = const_bf16("am0")
            m1, bm1 = const_bf16("am1")
            m0f, bm0f = const_bf16("am0f")
            perm, bperm = const_bf16("perm")
            ones = _alloc(C, sb, "ones", [128, 128], BF16)
            bones = Buf("ones")
            S.op("vector", lambda e: e.memset(ones[:], 1.0), writes=[bones])
            ropeC = _alloc(C, sb, "ropeC", [128, 4096], F32)
            ropeS = _alloc(C, sb, "ropeS", [128, 4096], F32)
            brope = Buf("rope")
            S.dma("sync", ropeC[:], C.dram["ropeC"], writes=[brope])
            S.dma("sync", ropeS[:], C.dram["ropeS"], writes=[brope])

            xs = _ring(C, sb, "xs", [128, D], F32, 2)
            for t in range(NT):
                for (src, dst, bd) in ((C.dram["xh"], hTh, bhTh), (C.dram["x"], hTo, bhTo)):
                    xt, bxt = xs.next()
                    S.dma("sync", xt[:], src[t * 128:(t + 1) * 128, :], writes=[bxt])
                    norm_tile(C, N, xt[:], bxt, nwT[:], bnw, dst[:, :, t * 128:(t + 1) * 128], bd[t])

            o_acc = _alloc(C, sb, "o_acc", [128, TOK], F32)
            l_acc = _alloc(C, sb, "l_acc", [128, TOK], F32)
            bacc = Buf("acc")
            qT = _alloc(C, sb, "qT", [128, TOK], BF16)
            kT = _alloc(C, sb, "kT", [128, 4096], BF16)
            Vt = _alloc(C, sb, "Vt", [128, 32, 128], BF16)
            bq, bk, bv = Buf("qT"), Buf("kT"), Buf("Vt")
            wqr = _ring(C, sb, "wq", [128, 8, 128], BF16, 2)
            wkr = _ring(C, sb, "wk", [128, 8, 128], BF16, 2)
            wvr = _ring(C, sb, "wv", [128, 8, 128], BF16, 2)
            rawb = _ring(C, sb, "rawb", [128, 512], BF16, 2)
            t1r = _ring(C, sb, "t1", [128, 512], F32, 1)
            t2r = _ring(C, sb, "t2", [128, 512], F32, 1)
            ptr = _ring(C, sb, "pt", [128, 4, 128], BF16, 3)
            pS = _ring(C, sb, "pS", [128, 4, 128], F32, 2, psum=True)
            pO = _ring(C, sb, "pO", [128, 4, 128], F32, 1, psum=True)
            pL = _ring(C, sb, "pL", [128, 4, 128], F32, 1, psum=True)

            def load_w(h, g):
                out = []
                for j, ring in enumerate((wqr, wkr, wvr)):
                    wt, bw = ring.next()
                    c0 = g * 3072 + j * 1024 + h * 128
                    for k in range(8):
                        S.dma("gpsimd", wt[:, k, :], w_in[k * 128:(k + 1) * 128, c0:c0 + 128], writes=[bw])
                    out.append((wt, bw))
                return out

            def proj_rope(wt, bw, src, bsrc_list, n, pos0, dst_ap, bdst, dil):
                praw, bpraw = pp.next()
                for k in range(8):
                    S.op("tensor", lambda e, k=k: e.matmul(praw[:, 0:n], lhsT=wt[:, k, :], rhs=src[:, k, :],
                                                           start=(k == 0), stop=(k == 7)),
                         reads=[bw] + bsrc_list, writes=[bpraw], signal=(k == 7))
                rb, brb = rawb.next()
                S.op("scalar", lambda e: e.copy(out=rb[:, 0:n], in_=praw[:, 0:n]), reads=[bpraw], writes=[brb])
                pswp, bpswp = pp.next()
                S.op("tensor", lambda e: e.matmul(pswp[:, 0:n], lhsT=perm[:], rhs=rb[:, 0:n], start=True, stop=True),
                     reads=[brb, bperm], writes=[bpswp])
                t1, bt1 = t1r.next()
                t2, bt2 = t2r.next()
                S.op("vector", lambda e: e.tensor_tensor(out=t1[:, 0:n], in0=praw[:, 0:n], in1=ropeC[:, pos0:pos0 + n], op=ALU.mult),
                     reads=[bpraw, brope], writes=[bt1])
                S.op("vector", lambda e: e.tensor_tensor(out=t2[:, 0:n], in0=pswp[:, 0:n], in1=ropeS[:, pos0:pos0 + n], op=ALU.mult),
                     reads=[bpswp, brope], writes=[bt2])
                if dil == 1:
                    i0, i1 = t1[:, 0:n], t2[:, 0:n]
                else:
                    i0 = t1[:, 0:n].rearrange("p (s r) -> p r s", r=dil)
                    i1 = t2[:, 0:n].rearrange("p (s r) -> p r s", r=dil)
                S.op("vector", lambda e: e.tensor_tensor(out=dst_ap, in0=i0, in1=i1, op=ALU.add),
                     reads=[bt1, bt2], writes=[bdst])

            wl = [load_w(0, 0)]
            it = 0
            for h in range(C.dbg.get("heads", 8)):
                for g in C.dbg.get("groups", (0, 1, 2)):
                    dil = DILS[g]
                    nh = 128 * dil
                    nb = 16 // dil
                    L = 128 * (nb + 1)
                    if C.dbg:
                        (wq, bwq), (wk, bwk), (wv, bwv) = load_w(h, g)
                    else:
                        (wq, bwq), (wk, bwk), (wv, bwv) = wl[it]
                        it += 1
                        nxt = (h, g + 1) if g < 2 else (h + 1, 0)
                        if nxt[0] < 8:
                            wl.append(load_w(*nxt))
                    kTr = kT[:, 0:dil * L].rearrange("p (r l) -> p r l", r=dil)
                    qTr = qT[:, :].rearrange("p (r l) -> p r l", r=dil)
                    hchunks = [(j0, min(512, nh)) for j0 in range(2048 - nh, 2048, 512)]
                    for (j0, n) in hchunks:
                        c0 = j0 - (2048 - nh)
                        dst = kTr[:, :, c0 // dil:(c0 + n) // dil] if dil > 1 else kT[:, c0:c0 + n]
                        proj_rope(wk, bwk, hTh[:, :, j0:j0 + n], bhTh[j0 // 128:(j0 + n) // 128], n, j0, dst, bk, dil)
                    for a in range(4):
                        c0 = nh + a * 512
                        dst = kTr[:, :, c0 // dil:(c0 + 512) // dil] if dil > 1 else kT[:, c0:c0 + 512]
                        proj_rope(wk, bwk, hTo[:, :, a * 512:(a + 1) * 512], bhTo[a * 4:(a + 1) * 4], 512, 2048 + a * 512, dst, bk, dil)
                    for a in range(4):
                        c0 = a * 512
                        dst = qTr[:, :, c0 // dil:(c0 + 512) // dil] if dil > 1 else qT[:, c0:c0 + 512]
                        proj_rope(wq, bwq, hTo[:, :, a * 512:(a + 1) * 512], bhTo[a * 4:(a + 1) * 4], 512, 2048 + a * 512, dst, bq, dil)
                    if C.dbg.get("stop", 9) <= 1:
                        continue
                    vtiles = []
                    for r in range(dil):
                        for m in range(nb + 1):
                            if m == 0:
                                sl = slice(2048 - nh + r, 2048, dil)
                                vtiles.append((hTh, bhTh, sl))
                            else:
                                s0 = r + 128 * dil * (m - 1)
                                sl = slice(s0, s0 + 127 * dil + 1, dil)
                                vtiles.append((hTo, bhTo, sl))
                    for v0 in range(0, len(vtiles), 4):
                        grp = vtiles[v0:v0 + 4]
                        pv, bpv = pp.next()
                        for j, (hsrc, bh, sl) in enumerate(grp):
                            for k in range(8):
                                S.op("tensor", lambda e, j=j, k=k, hsrc=hsrc, sl=sl, pv=pv, wv=wv: e.matmul(
                                    pv[:, j * 128:(j + 1) * 128], lhsT=hsrc[:, k, sl], rhs=wv[:, k, :],
                                    start=(k == 0), stop=(k == 7)),
                                    reads=[bwv] + bh, writes=[bpv], signal=(k == 7))
                        ng = len(grp)
                        S.op("scalar", lambda e, pv=pv, v0=v0, ng=ng: e.copy(
                            out=Vt[:, v0:v0 + ng, :], in_=pv[:, 0:ng * 128].rearrange("p (j d) -> p j d", d=128)),
                            reads=[bpv], writes=[bv])
                    if C.dbg.get("stop", 9) <= 2:
                        continue
                    oa4 = o_acc[:, :].rearrange("p (b i r) -> p b r i", i=128, r=dil)
                    la4 = l_acc[:, :].rearrange("p (b i r) -> p b r i", i=128, r=dil)
                    blocks = [(b, r) for b in range(nb) for r in range(dil)]
                    for q0 in range(0, 16, 4):
                        quad = blocks[q0:q0 + 4]
                        po, bpo = pO.next()
                        pl, bpl = pL.next()
                        for half in range(2):
                            ps, bps = pS.next()
                            pt, bpt = ptr.next()
                            for jj in range(2):
                                b, r = quad[half * 2 + jj]
                                qv = qTr[:, r, 128 * b:128 * (b + 1)]
                                for ch in range(2):
                                    kv = kTr[:, r, 128 * (b + ch):128 * (b + ch + 1)]
                                    mk, bmk = (m1, bm1) if ch == 1 else ((m0f, bm0f) if b == 0 else (m0, bm0))
                                    S.op("tensor", lambda e, ps=ps, jj=jj, ch=ch, kv=kv, qv=qv: e.matmul(
                                        ps[:, jj * 2 + ch, :], lhsT=kv, rhs=qv, start=True, stop=False),
                                        reads=[bk, bq], writes=[bps], signal=False)
                                    S.op("tensor", lambda e, ps=ps, jj=jj, ch=ch, mk=mk: e.matmul(
                                        ps[:, jj * 2 + ch, :], lhsT=C.identb[:], rhs=mk[:], start=False, stop=True),
                                        reads=[bmk], writes=[bps], signal=(jj == 1 and ch == 1))
                            S.op("scalar", lambda e, ps=ps, pt=pt: e.activation(out=pt[:], in_=ps[:], func=AF.Exp, scale=QSCALE),
                                 reads=[bps], writes=[bpt])
                            for jj in range(2):
                                b, r = quad[half * 2 + jj]
                                bi = half * 2 + jj
                                for ch in range(2):
                                    vt = r * (nb + 1) + b + ch
                                    S.op("tensor", lambda e, po=po, bi=bi, vt=vt, pt=pt, jj=jj, ch=ch: e.matmul(
                                        po[:, bi, :], lhsT=Vt[:, vt, :], rhs=pt[:, jj * 2 + ch, :], start=(ch == 0), stop=(ch == 1)),
                                        reads=[bv, bpt], writes=[bpo], signal=False)
                                for ch in range(2):
                                    S.op("tensor", lambda e, pl=pl, bi=bi, pt=pt, jj=jj, ch=ch: e.matmul(
                                        pl[:, bi, :], lhsT=ones[:], rhs=pt[:, jj * 2 + ch, :], start=(ch == 0), stop=(ch == 1)),
                                        reads=[bones, bpt], writes=[bpl], signal=(ch == 1))
                        if dil == 1:
                            od, ld = oa4[:, q0:q0 + 4, 0, :], la4[:, q0:q0 + 4, 0, :]
                        elif dil == 4:
                            od, ld = oa4[:, q0 // 4, :, :], la4[:, q0 // 4, :, :]
                        else:
                            od, ld = oa4[:, 0, q0:q0 + 4, :], la4[:, 0, q0:q0 + 4, :]
                        if g == C.dbg.get("groups", (0,))[0]:
                            S.op("vector", lambda e, od=od, po=po: e.tensor_copy(out=od, in_=po[:]), reads=[bpo], writes=[bacc])
                            S.op("vector", lambda e, ld=ld, pl=pl: e.tensor_copy(out=ld, in_=pl[:]), reads=[bpl], writes=[bacc])
                        else:
                            S.op("vector", lambda e, od=od, po=po: e.tensor_tensor(out=od, in0=po[:], in1=od, op=ALU.add),
                                 reads=[bpo, bacc], writes=[bacc])
                            S.op("vector", lambda e, ld=ld, pl=pl: e.tensor_tensor(out=ld, in0=pl[:], in1=ld, op=ALU.add),
                                 reads=[bpl, bacc], writes=[bacc])
                S.op("vector", lambda e: e.reciprocal(out=l_acc[:], in_=l_acc[:]), reads=[bacc], writes=[bacc])
                S.op("vector", lambda e, h=h: e.tensor_tensor(out=oT[:, h, :], in0=o_acc[:], in1=l_acc[:], op=ALU.mult),
                     reads=[bacc], writes=[boT[h]])
            if C.dbg.get("dump"):
                bd = Buf("dump")
                for nm, t_ in (("d_kT", kT), ("d_qT", qT), ("d_oacc", o_acc), ("d_lacc", l_acc)):
                    S.dma("sync", C.dram[nm], t_[:], reads=[bk, bq, bacc], writes=[bd])
                S.dma("sync", C.dram["d_Vt"], Vt[:].rearrange("p t d -> p (t d)"), reads=[bv], writes=[bd])
                S.dma("sync", C.dram["d_oT"], oT[:].rearrange("p h t -> p (h t)"), reads=boT, writes=[bd])
            S.barrier()

        with ExitStack() as sc:
            wo = _alloc(C, sc, "wo", [128, 8, D], BF16)
            bwo = Buf("wo")
            for hh in range(8):
                S.dma("gpsimd", wo[:, hh, :], w_out[hh * 128:(hh + 1) * 128, :], writes=[bwo])
            for t in range(NT):
                S.dma("sync", C.x_sb[:, t, :], C.dram["x"][t * 128:(t + 1) * 128, :], writes=[C.bx[t]])
            for t in range(NT):
                for c in range(2):
                    bank, bbank = pp.next()
                    for hh in range(8):
                        S.op("tensor", lambda e, hh=hh, t=t, c=c, bank=bank: e.matmul(
                            bank[:], lhsT=oT[:, hh, t * 128:(t + 1) * 128], rhs=wo[:, hh, c * 512:(c + 1) * 512],
                            start=(hh == 0), stop=(hh == 7)),
                            reads=[bwo] + boT, writes=[bbank], signal=(hh == 7))
                    xsl = C.x_sb[:, t, c * 512:(c + 1) * 512]
                    S.op("vector", lambda e, xsl=xsl, bank=bank: e.tensor_tensor(out=xsl, in0=bank[:], in1=xsl, op=ALU.add),
                         reads=[bbank, C.bx[t]], writes=[C.bx[t]])
            S.barrier()


def phase_gla(C, mode):
    S, nc = C.S, C.nc
    full = (mode == "main")
    w_in = C.dram["gla_w_in"]
    with ExitStack() as st:
        pb = _ring(C, st, "pb", [128, 512], F32, 4, psum=True)
        pw = _ring(C, st, "pw", [128, 1024], F32, 2, psum=True)

        def bfview(bank):
            return bank[:].bitcast(BF16).rearrange("p (k n) -> p k n", n=128)

        N = Ctx()
        N.junk = _ring(C, st, "njunk", [128, D], BF16, 1)
        N.ss = _ring(C, st, "nss", [128, 1], F32, 4)
        N.xn = _ring(C, st, "nxn", [128, D], BF16, 2)
        N.pT = Ring([(bfview(b), bb) for (b, bb) in pb.items])
        nwT, bnw = load_normw(C, st, C.dram["norm_mix_w"][1], "nw_mix1")

        win = _alloc(C, st, "g_win", [128, 8, 3088], BF16)
        bwin = Buf("g_win")
        for k in range(8):
            S.dma("gpsimd", win[:, k, :], w_in[k * 128:(k + 1) * 128, :], writes=[bwin])
        wgu = _alloc(C, st, "g_wgu", [32, 512], BF16)
        bwgu = Buf("g_wgu")
        S.op("vector", lambda e: e.memset(wgu[:], 0.0), writes=[bwgu])
        S.dma("gpsimd", wgu[0:16, :], C.dram["gla_w_gate_up"], writes=[bwgu])
        S.dma("gpsimd", wgu[16:17, :], C.dram["gla_b_gate"].rearrange("(o n) -> o n", o=1), writes=[bwgu])
        rmask = _alloc(C, st, "g_rmask", [128, 512], F32)
        brm = Buf("g_rmask")
        S.dma("sync", rmask[:], C.dram["scanmask"], writes=[brm])
        glr = _alloc(C, st, "g_glr", [32, 128], BF16)
        bglr = Buf("g_glr")
        S.op("vector", lambda e: e.memset(glr[:], 1.0), writes=[bglr])
        Sst = _alloc(C, st, "g_S", [128, 4, 256], F32)
        Sbf = _alloc(C, st, "g_Sbf", [128, 4, 256], BF16)
        bS, bSbf = Buf("g_S"), Buf("g_Sbf")
        cstot = _alloc(C, st, "g_cstot", [128, 4], F32)
        bct = Buf("g_cstot")
        if full:
            wout = _alloc(C, st, "g_wout", [128, 8, D], BF16)
            bwout = Buf("g_wout")
            for k in range(8):
                S.dma("gpsimd", wout[:, k, :], C.dram["gla_w_out"][k * 128:(k + 1) * 128, :], writes=[bwout])
            causal = _alloc(C, st, "g_causal", [128, 128], F32)
            bcau = Buf("g_causal")
            S.dma("sync", causal[:], C.dram["causal"], writes=[bcau])
            gnw = _alloc(C, st, "g_gnw", [128, 256], F32)
            bgnw = Buf("g_gnw")
            S.dma("sync", gnw[:], C.dram["gla_norm_w"].partition_broadcast(128), writes=[bgnw])
            sel = _alloc(C, st, "g_sel", [128, 8], F32)
            bsel = Buf("g_sel")
            S.dma("sync", sel[:], C.dram["sel"], writes=[bsel])
            Tt = _alloc(C, st, "g_T", [128, 4, 256], F32)
            bT = Buf("g_T")
            sjr = _ring(C, st, "g_sj", [128, 4, 256], F32, 2)
            djr = _ring(C, st, "g_dj", [128, 4], F32, 2)
            S.op("vector", lambda e: e.memset(Sst[:], 0.0), writes=[bS])
            S.op("vector", lambda e: e.memset(Tt[:], 0.0), writes=[bT])
            for j in range(8):
                S.op("vector", lambda e, j=j: e.scalar_tensor_tensor(
                    out=Sst[:].rearrange("p h v -> p (h v)"), in0=Tt[:].rearrange("p h v -> p (h v)"),
                    scalar=sel[:, j:j + 1], in1=Sst[:].rearrange("p h v -> p (h v)"), op0=ALU.mult, op1=ALU.add),
                    reads=[bT, bsel, bS], writes=[bS])
                if j % 4 == 3:
                    if j < 7:
                        S.op("vector", lambda e: e.memset(Tt[:], 0.0), reads=[], writes=[bT])
                    continue
                sj, bsj = sjr.next()
                dj, bdj = djr.next()
                S.dma("sync", sj[:].rearrange("p h v -> p (h v)"), C.dram["gS_all"][j], writes=[bsj])
                S.dma("sync", dj[:], C.dram["gD_all"][j], writes=[bdj])
                S.op("scalar", lambda e, dj=dj: e.activation(out=dj[:], in_=dj[:], func=AF.Exp, scale=-1.0 / 16.0),
                     reads=[bdj], writes=[bdj])
                for h in range(4):
                    S.op("vector", lambda e, h=h, sj=sj, dj=dj: e.scalar_tensor_tensor(
                        out=Tt[:, h, :], in0=Tt[:, h, :], scalar=dj[:, h:h + 1], in1=sj[:, h, :], op0=ALU.mult, op1=ALU.add),
                        reads=[bT, bsj, bdj], writes=[bT])
        else:
            S.op("vector", lambda e: e.memset(Sst[:], 0.0), writes=[bS])
            S.op("vector", lambda e: e.memset(cstot[:], 0.0), writes=[bct])
        if full:
            S.op("scalar", lambda e: e.copy(out=Sbf[:], in_=Sst[:]), reads=[bS], writes=[bSbf])

        hTr = _ring(C, st, "g_hT", [128, 8, 128], BF16, 2)
        g1 = _alloc(C, st, "g_g1", [128, 4, 128], F32); bg1 = Buf("g1")
        g2 = _alloc(C, st, "g_g2", [128, 4, 128], F32); bg2 = Buf("g2")
        g3 = _alloc(C, st, "g_g3", [128, 4, 128], F32); bg3 = Buf("g3")
        dec = _alloc(C, st, "g_dec", [128, 4], F32); bdec = Buf("dec")
        kdT = _alloc(C, st, "g_kdT", [128, 4, 128], BF16); bkdT = Buf("kdT")
        kd = _alloc(C, st, "g_kd", [128, 4, 128], BF16); bkd = Buf("kd")
        Vr = _ring(C, st, "g_V", [128, 1024], BF16, 2)
        if full:
            qt = _alloc(C, st, "g_qt", [128, 4, 128], BF16); bqt = Buf("qt")
            kt = _alloc(C, st, "g_kt", [128, 4, 128], BF16); bkt = Buf("kt")
            At = _alloc(C, st, "g_At", [128, 4, 128], BF16); bAt = Buf("At")
            sg = _alloc(C, st, "g_sg", [128, 1024], BF16); bsg = Buf("sg")
            ot = _alloc(C, st, "g_ot", [128, 4, 256], F32); bot = Buf("ot")
            og = _alloc(C, st, "g_og", [128, 1024], BF16); bog = Buf("og")
            ogT = _alloc(C, st, "g_ogT", [128, 8, 128], BF16); bogT = Buf("ogT")
            ms = _alloc(C, st, "g_ms", [128, 4], F32); bms = Buf("ms")
            ojunk = _alloc(C, st, "g_oj", [128, 256], BF16); boj = Buf("oj")

        for t in range(NT):
            hT, bhT = hTr.next()
            norm_tile(C, N, C.x_sb[:, t, :], C.bx[t], nwT[:], bnw, hT[:], bhT)
            pgl, bpgl = pb.next()
            for k in range(8):
                S.op("tensor", lambda e, k=k, pgl=pgl, hT=hT: e.matmul(pgl[0:16, 0:128], lhsT=win[:, k, 3072:3088], rhs=hT[:, k, :],
                                                                    start=(k == 0), stop=(k == 7)),
                     reads=[bwin, bhT], writes=[bpgl], signal=(k == 7))
            S.op("scalar", lambda e, pgl=pgl: e.copy(out=glr[0:16, :], in_=pgl[0:16, 0:128]), reads=[bpgl], writes=[bglr])
            pg, bpg = pb.next()
            for h in range(4):
                S.op("tensor", lambda e, h=h, pg=pg: e.matmul(pg[:, h * 128:(h + 1) * 128], lhsT=wgu[:, h * 128:(h + 1) * 128], rhs=glr[:],
                                                              start=True, stop=True),
                     reads=[bwgu, bglr], writes=[bpg], signal=(h == 3))
            pk, bpk = pb.next()
            for h in range(4):
                for k in range(8):
                    S.op("tensor", lambda e, h=h, k=k, pk=pk, hT=hT: e.matmul(
                        pk[:, h * 128:(h + 1) * 128], lhsT=win[:, k, 512 + h * 128:512 + (h + 1) * 128], rhs=hT[:, k, :],
                        start=(k == 0), stop=(k == 7)),
                        reads=[bwin, bhT], writes=[bpk], signal=(h == 3 and k == 7))
            if full:
                pq, bpq = pb.next()
                for h in range(4):
                    for k in range(8):
                        S.op("tensor", lambda e, h=h, k=k, pq=pq, hT=hT: e.matmul(
                            pq[:, h * 128:(h + 1) * 128], lhsT=win[:, k, h * 128:(h + 1) * 128], rhs=hT[:, k, :],
                            start=(k == 0), stop=(k == 7)),
                            reads=[bwin, bhT], writes=[bpq], signal=(h == 3 and k == 7))
            pv, bpv = pw.next()
            for c in range(2):
                for k in range(8):
                    S.op("tensor", lambda e, c=c, k=k, pv=pv, hT=hT: e.matmul(
                        pv[:, c * 512:(c + 1) * 512], lhsT=hT[:, k, :], rhs=win[:, k, 1024 + c * 512:1024 + (c + 1) * 512],
                        start=(k == 0), stop=(k == 7)),
                        reads=[bwin, bhT], writes=[bpv], signal=(c == 1 and k == 7))
            V, bV = Vr.next()
            S.op("scalar", lambda e, V=V, pv=pv: e.copy(out=V[:], in_=pv[:]), reads=[bpv], writes=[bV])
            if full:
                pgo, bpgo = pw.next()
                for c in range(2):
                    for k in range(8):
                        S.op("tensor", lambda e, c=c, k=k, pgo=pgo, hT=hT: e.matmul(
                            pgo[:, c * 512:(c + 1) * 512], lhsT=hT[:, k, :], rhs=win[:, k, 2048 + c * 512:2048 + (c + 1) * 512],
                            start=(k == 0), stop=(k == 7)),
                            reads=[bwin, bhT], writes=[bpgo], signal=(c == 1 and k == 7))
                S.op("scalar", lambda e, pgo=pgo: e.activation(out=sg[:], in_=pgo[:], func=AF.Silu), reads=[bpgo], writes=[bsg])
            g1f = g1[:].rearrange("p h n -> p (h n)")
            g2f = g2[:].rearrange("p h n -> p (h n)")
            g3f = g3[:].rearrange("p h n -> p (h n)")
            S.op("scalar", lambda e, pg=pg: e.activation(out=g1f, in_=pg[:], func=AF.Exp, scale=-1.0), reads=[bpg], writes=[bg1])
            S.op("scalar", lambda e: e.activation(out=g1f, in_=g1f, func=AF.Ln, bias=1.0), reads=[bg1], writes=[bg1])
            S.op("vector", lambda e: e.tensor_tensor_scan(out=g2f, data0=rmask[:], data1=g1f, initial=0.0, op0=ALU.mult, op1=ALU.add),
                 reads=[bg1, brm], writes=[bg2])
            S.op("scalar", lambda e: e.activation(out=dec[:], in_=g2[:, :, 127], func=AF.Exp, scale=-1.0 / 16.0),
                 reads=[bg2], writes=[bdec])
            if not full:
                S.op("vector", lambda e: e.tensor_tensor(out=cstot[:], in0=cstot[:], in1=g2[:, :, 127], op=ALU.add),
                     reads=[bg2, bct], writes=[bct])
            S.op("vector", lambda e: e.tensor_tensor(out=g3[:], in0=g2[:], in1=g2[:, :, 127:128].to_broadcast([128, 4, 128]),
                                                     op=ALU.subtract), reads=[bg2], writes=[bg3])
            S.op("scalar", lambda e: e.activation(out=g3f, in_=g3f, func=AF.Exp, scale=1.0 / 16.0), reads=[bg3], writes=[bg3])
            S.op("vector", lambda e, pk=pk: e.tensor_tensor(out=kdT[:].rearrange("p h n -> p (h n)"), in0=pk[:], in1=g3f, op=ALU.mult),
                 reads=[bpk, bg3], writes=[bkdT])
            if full:
                S.op("scalar", lambda e: e.activation(out=g3f, in_=g2f, func=AF.Exp, scale=1.0 / 16.0), reads=[bg2, bkdT], writes=[bg3])
                S.op("vector", lambda e, pk=pk: e.tensor_tensor(out=kt[:].rearrange("p h n -> p (h n)"), in0=pk[:], in1=g3f, op=ALU.mult),
                     reads=[bpk, bg3], writes=[bkt])
                S.op("scalar", lambda e: e.activation(out=g1f, in_=g2f, func=AF.Exp, scale=-1.0 / 16.0), reads=[bg2], writes=[bg1])
                S.op("vector", lambda e, pq=pq: e.scalar_tensor_tensor(out=qt[:].rearrange("p h n -> p (h n)"), in0=pq[:], scalar=QSCALE,
                                                                       in1=g1f, op0=ALU.mult, op1=ALU.mult),
                     reads=[bpq, bg1], writes=[bqt])
            ptk, bptk = pb.next()
            ptkv = bfview(ptk)
            for h in range(4):
                S.op("tensor", lambda e, h=h, ptkv=ptkv: e.transpose(out=ptkv[:, h, :], in_=kdT[:, h, :], identity=C.identb[:]),
                     reads=[bkdT], writes=[bptk], signal=(h == 3))
            S.op("scalar", lambda e, ptkv=ptkv: e.copy(out=kd[:], in_=ptkv[:, 0:4, :]), reads=[bptk], writes=[bkd])
            if full:
                pa, bpa = pb.next()
                for h in range(4):
                    S.op("tensor", lambda e, h=h, pa=pa: e.matmul(pa[:, h * 128:(h + 1) * 128], lhsT=kt[:, h, :], rhs=qt[:, h, :],
                                                                  start=True, stop=True),
                         reads=[bkt, bqt], writes=[bpa], signal=(h == 3))
                S.op("vector", lambda e, pa=pa: e.tensor_tensor(
                    out=At[:], in0=pa[:].rearrange("p (h n) -> p h n", n=128),
                    in1=causal[:].unsqueeze(1).to_broadcast([128, 4, 128]), op=ALU.mult),
                    reads=[bpa, bcau], writes=[bAt])
                po, bpo = pw.next()
                for h in range(4):
                    S.op("tensor", lambda e, h=h, po=po, V=V: e.matmul(po[:, h * 256:(h + 1) * 256], lhsT=At[:, h, :], rhs=V[:, h * 256:(h + 1) * 256],
                                                                       start=True, stop=False),
                         reads=[bAt, bV], writes=[bpo], signal=False)
                    S.op("tensor", lambda e, h=h, po=po: e.matmul(po[:, h * 256:(h + 1) * 256], lhsT=qt[:, h, :], rhs=Sbf[:, h, :],
                                                                  start=False, stop=True),
                         reads=[bqt, bSbf], writes=[bpo], signal=(h == 3))
            pu, bpu = pw.next()
            for h in range(4):
                S.op("tensor", lambda e, h=h, pu=pu, V=V: e.matmul(pu[:, h * 256:(h + 1) * 256], lhsT=kd[:, h, :], rhs=V[:, h * 256:(h + 1) * 256],
                                                                   start=True, stop=True),
                     reads=[bkd, bV], writes=[bpu], signal=(h == 3))
            for h in range(4):
                S.op("vector", lambda e, h=h, pu=pu: e.scalar_tensor_tensor(
                    out=Sst[:, h, :], in0=Sst[:, h, :], scalar=dec[:, h:h + 1], in1=pu[:, h * 256:(h + 1) * 256], op0=ALU.mult, op1=ALU.add),
                    reads=[bS, bdec, bpu] + ([bSbf] if full else []), writes=[bS])
            if full:
                S.op("scalar", lambda e: e.copy(out=Sbf[:], in_=Sst[:]), reads=[bS, bpo], writes=[bSbf])
                for h in range(4):
                    S.op("scalar", lambda e, h=h, po=po: e.activation(out=ojunk[:], in_=po[:, h * 256:(h + 1) * 256], func=AF.Square,
                                                                      accum_out=ms[:, h:h + 1]),
                         reads=[bpo], writes=[boj, bms])
                S.op("scalar", lambda e: e.activation(out=ms[:], in_=ms[:], func=AF.Sqrt, scale=1.0 / 256.0, bias=C.eps[:]),
                     reads=[bms], writes=[bms])
                S.op("vector", lambda e: e.reciprocal(out=ms[:], in_=ms[:]), reads=[bms], writes=[bms])
                S.op("vector", lambda e, po=po: e.tensor_tensor(out=ot[:], in0=po[:].rearrange("p (h v) -> p h v", v=256),
                                                                in1=ms[:].unsqueeze(2).to_broadcast([128, 4, 256]), op=ALU.mult),
                     reads=[bpo, bms], writes=[bot])
                S.op("vector", lambda e: e.tensor_tensor(out=ot[:], in0=ot[:], in1=gnw[:].unsqueeze(1).to_broadcast([128, 4, 256]), op=ALU.mult),
                     reads=[bot, bgnw], writes=[bot])
                S.op("vector", lambda e: e.tensor_tensor(out=og[:], in0=ot[:].rearrange("p h v -> p (h v)"), in1=sg[:], op=ALU.mult),
                     reads=[bot, bsg], writes=[bog])
                pto, bpto = pb.next()
                ptov = bfview(pto)
                for k in range(8):
                    S.op("tensor", lambda e, k=k, ptov=ptov: e.transpose(out=ptov[:, k, :], in_=og[:, k * 128:(k + 1) * 128], identity=C.identb[:]),
                         reads=[bog], writes=[bpto], signal=(k == 7))
                S.op("scalar", lambda e, ptov=ptov: e.copy(out=ogT[:], in_=ptov), reads=[bpto], writes=[bogT])
                pox, bpox = pw.next()
                for c in range(2):
                    for k in range(8):
                        S.op("tensor", lambda e, c=c, k=k, pox=pox: e.matmul(
                            pox[:, c * 512:(c + 1) * 512], lhsT=ogT[:, k, :], rhs=wout[:, k, c * 512:(c + 1) * 512],
                            start=(k == 0), stop=(k == 7)),
                            reads=[bogT, bwout], writes=[bpox], signal=(c == 1 and k == 7))
                xsl = C.x_sb[:, t, :]
                S.op("vector", lambda e, xsl=xsl, pox=pox: e.tensor_tensor(out=xsl, in0=pox[:], in1=xsl, op=ALU.add),
                     reads=[bpox, C.bx[t]], writes=[C.bx[t]])
        if not full:
            bgo = Buf("gS_out")
            S.dma("sync", C.dram["gS"], Sst[:].rearrange("p h v -> p (h v)"), reads=[bS], writes=[bgo])
            S.dma("sync", C.dram["gD"], cstot[:], reads=[bct], writes=[bgo])
        S.barrier()


WSHAPES = {
    "norm_mix_w": [2, D], "norm_mlp_w": [2, D], "final_norm_w": [D],
    "attn_w_in": [D, 9216], "attn_w_out": [D, D],
    "gla_w_in": [D, 3088], "gla_w_gate_up": [16, 512], "gla_b_gate": [512], "gla_norm_w": [256],
    "gla_w_out": [D, D], "mlp_w_up": [2, D, 4096], "mlp_w_down": [2, 4096, D],
}


def build(phases, load_x=True, store_x=False, dbg=None):
    nc = bass.Bass("TRN2", target_bir_lowering=False)
    C = Ctx()
    C.dbg = dbg or {}
    C.nc = nc
    C.uid = 0
    C.dram = {}
    C.dram["x"] = nc.dram_tensor("x", [TOK, D], F32, kind="ExternalInput").ap()
    C.dram["ident"] = nc.dram_tensor("ident", [128, 128], F32, kind="ExternalInput").ap()
    if "attn" in phases:
        C.dram["xh"] = nc.dram_tensor("xh", [TOK, D], F32, kind="ExternalInput").ap()
        for nm in ("am0", "am1", "am0f", "perm"):
            C.dram[nm] = nc.dram_tensor(nm, [128, 128], F32, kind="ExternalInput").ap()
        C.dram["ropeC"] = nc.dram_tensor("ropeC", [128, 4096], F32, kind="ExternalInput").ap()
        C.dram["ropeS"] = nc.dram_tensor("ropeS", [128, 4096], F32, kind="ExternalInput").ap()
    need = {"norm_mix_w", "norm_mlp_w", "final_norm_w"}
    if "attn" in phases:
        need |= {"attn_w_in", "attn_w_out"}
    if "gla_pre" in phases or "gla_main" in phases:
        need |= {"gla_w_in", "gla_w_gate_up", "gla_b_gate", "gla_norm_w", "gla_w_out"}
    if "mlp0" in phases or "mlp1" in phases:
        need |= {"mlp_w_up", "mlp_w_down"}
    C.need = need
    for k, shp in WSHAPES.items():
        if k in need:
            C.dram[k] = nc.dram_tensor(k, shp, F32, kind="ExternalInput").ap()
    if "gla_pre" in phases or "gla_main" in phases:
        C.dram["scanmask"] = nc.dram_tensor("scanmask", [128, 512], F32, kind="ExternalInput").ap()
    if "gla_pre" in phases and "gla_main" not in phases:
        C.dram["gS"] = nc.dram_tensor("gS", [128, 1024], F32, kind="ExternalOutput").ap()
        C.dram["gD"] = nc.dram_tensor("gD", [128, 4], F32, kind="ExternalOutput").ap()
    if "gla_main" in phases:
        C.dram["causal"] = nc.dram_tensor("causal", [128, 128], F32, kind="ExternalInput").ap()
        C.dram["sel"] = nc.dram_tensor("sel", [128, 8], F32, kind="ExternalInput").ap()
        if "gla_pre" not in phases:
            C.dram["gS_all"] = nc.dram_tensor("gS_all", [8, 128, 1024], F32, kind="ExternalInput").ap()
            C.dram["gD_all"] = nc.dram_tensor("gD_all", [8, 128, 4], F32, kind="ExternalInput").ap()
    if C.dbg.get("dump"):
        C.dram["d_kT"] = nc.dram_tensor("d_kT", [128, 4096], BF16, kind="ExternalOutput").ap()
        C.dram["d_qT"] = nc.dram_tensor("d_qT", [128, 2048], BF16, kind="ExternalOutput").ap()
        C.dram["d_Vt"] = nc.dram_tensor("d_Vt", [128, 4096], BF16, kind="ExternalOutput").ap()
        C.dram["d_oacc"] = nc.dram_tensor("d_oacc", [128, 2048], F32, kind="ExternalOutput").ap()
        C.dram["d_oT"] = nc.dram_tensor("d_oT", [128, 8 * 2048], BF16, kind="ExternalOutput").ap()
        C.dram["d_lacc"] = nc.dram_tensor("d_lacc", [128, 2048], F32, kind="ExternalOutput").ap()
    C.dram["out"] = nc.dram_tensor("out", [TOK, D], F32, kind="ExternalOutput").ap()
    C.bout = Buf("out")
    with ExitStack() as st:
        S = Sched(nc, st)
        C.S = S
        C.x_sb = _alloc(C, st, "x_sb", [128, NT, D], F32)
        C.bx = [Buf(f"x{t}") for t in range(NT)]
        identf = _alloc(C, st, "identf", [128, 128], F32)
        C.identb = _alloc(C, st, "identb", [128, 128], BF16)
        C.eps = _alloc(C, st, "eps", [128, 1], F32)
        bi = Buf("identf")
        S.dma("sync", identf[:], C.dram["ident"], writes=[bi])
        S.op("vector", lambda e: e.tensor_copy(out=C.identb[:], in_=identf[:]), reads=[bi], writes=[Buf("identb")])
        S.op("vector", lambda e: e.memset(C.eps[:], EPS), writes=[Buf("eps")])
        if load_x and phases[0] != "attn":
            for t in range(NT):
                S.dma("sync", C.x_sb[:, t, :], C.dram["x"][t * 128:(t + 1) * 128, :], writes=[C.bx[t]])
        S.barrier()
        for ph in phases:
            if ph == "attn":
                phase_attn(C)
            elif ph == "gla_pre":
                phase_gla(C, "pre")
            elif ph == "gla_main":
                phase_gla(C, "main")
            elif ph == "mlp0":
                phase_mlp(C, 0)
            elif ph == "mlp1":
                phase_mlp(C, 1)
            elif ph == "final":
                phase_final(C)
            else:
                raise ValueError(ph)
        if store_x:
            for t in range(NT):
                S.dma("sync", C.dram["out"][t * 128:(t + 1) * 128, :], C.x_sb[:, t, :], reads=[C.bx[t]], writes=[C.bout])
        S.barrier()
        S.emit()
    return nc


def _attn_consts(c):
    m = {}
    j = np.arange(128)[:, None]
    i = np.arange(128)[None, :]
    m["am0"] = np.where(j >= i, 0.0, NEG).astype(np.float32)
    m["am1"] = np.where(j <= i, 0.0, NEG).astype(np.float32)
    m["am0f"] = m["am0"] if (c % 4) != 0 else np.full((128, 128), NEG, np.float32)
    pm = np.zeros((128, 128), np.float32)
    for mm in range(128):
        pm[(mm + 64) % 128, mm] = 1.0
    m["perm"] = pm
    inv = (10000.0 ** (-np.arange(0, 128, 2, dtype=np.float32) / 128)).astype(np.float32)
    pos = np.arange(4096, dtype=np.float32)
    ang = pos[None, :] * inv[:, None]
    cs, sn = np.cos(ang).astype(np.float32), np.sin(ang).astype(np.float32)
    m["ropeC"] = np.ascontiguousarray(np.concatenate([cs, cs], 0))
    m["ropeS"] = np.ascontiguousarray(np.concatenate([-sn, sn], 0))
    return m


def _gla_consts(c):
    m = {}
    sm = np.ones((128, 512), np.float32)
    sm[:, ::128] = 0.0
    m["scanmask"] = sm
    s = np.arange(128)[:, None]
    cc = np.arange(128)[None, :]
    m["causal"] = (s <= cc).astype(np.float32)
    sel = np.zeros((128, 8), np.float32)
    sel[:, c] = 1.0
    m["sel"] = sel
    return m


def _weights(inp):
    f = lambda a: np.ascontiguousarray(np.asarray(a, dtype=np.float32))
    return {
        "norm_mix_w": f(inp["norm_mix_w"]), "norm_mlp_w": f(inp["norm_mlp_w"]), "final_norm_w": f(inp["final_norm_w"]),
        "attn_w_in": f(inp["attn_w_in"][0]), "attn_w_out": f(inp["attn_w_out"][0]),
        "gla_w_in": f(inp["gla_w_in"][0]), "gla_w_gate_up": f(inp["gla_w_gate_up"][0]),
        "gla_b_gate": f(inp["gla_b_gate"][0]), "gla_norm_w": f(inp["gla_norm_w"][0]), "gla_w_out": f(inp["gla_w_out"][0]),
        "mlp_w_up": f(inp["mlp_w_up"]), "mlp_w_down": f(inp["mlp_w_down"]),
        "ident": np.eye(128, dtype=np.float32),
    }


_PROG = {}


def _prog(key, *a, **kw):
    if key not in _PROG:
        _PROG[key] = build(*a, **kw)
    return _PROG[key]


def kernel(**inputs):
    x = np.ascontiguousarray(np.asarray(inputs["x"], dtype=np.float32)).reshape(8, TOK, D)
    W = _weights(inputs)
    cores = list(range(8))
    nc1 = _prog("l1", ["attn", "mlp0", "gla_pre"], store_x=True)
    maps = []
    for c in cores:
        m = dict(W)
        m["x"] = x[c]
        m["xh"] = x[c - 1] if c % 4 else np.zeros((TOK, D), np.float32)
        m.update(_attn_consts(c))
        m["scanmask"] = _gla_consts(c)["scanmask"]
        maps.append(m)
    r1 = run_bass_kernel_spmd(nc1, maps, core_ids=cores)
    x2 = [np.asarray(r1.results[c]["out"]) for c in cores]
    gS = np.ascontiguousarray(np.stack([np.asarray(r1.results[c]["gS"]) for c in cores]))
    gD = np.ascontiguousarray(np.stack([np.asarray(r1.results[c]["gD"]) for c in cores]))
    nc2 = _prog("l2", ["gla_main", "mlp1", "final"])
    maps = []
    for c in cores:
        m = dict(W)
        m["x"] = x2[c]
        m.update(_gla_consts(c))
        m["gS_all"] = gS
        m["gD_all"] = gD
        maps.append(m)
    r2 = run_bass_kernel_spmd(nc2, maps, core_ids=cores)
    out = np.stack([np.asarray(r2.results[c]["out"]) for c in cores])
    return out.reshape(2, 4 * TOK, D).astype(np.float32)
```
